# Optimizing a Trainium2 kernel written in Bass

```python
import math
import jax, jax.numpy as jnp
from jax import lax
import numpy as np

D_MODEL = 1024
BATCH = 32
SEQ = 2048
DEPTH = 4

N_MIXERS = 2
N_CONV_LAYERS = (DEPTH + 1) // 2
N_LRU_LAYERS = DEPTH // 2
SC_WIDTH = 3
LRU_WIDTH = 1280
LRU_HEADS = 10
LRU_BLOCK = LRU_WIDTH // LRU_HEADS
LRU_CONV_WIDTH = 4
LRU_C = 8.0
FFN_HIDDEN = 2816
FFN_CONV_WIDTH = 3
LN_EPS = 1e-5
DEEPNORM_ALPHA = (2.0 * DEPTH) ** 0.25
DEEPNORM_BETA = (8.0 * DEPTH) ** -0.25

kernel_name = "hybrid_shortconv_rglru_convffn_deepnorm"


def causal_dwconv(x, w, b):
    k_width = w.shape[0]
    s = x.shape[1]
    xp = jnp.pad(x, ((0, 0), (k_width - 1, 0), (0, 0)))
    y = xp[:, 0:s] * w[0] + b
    for k in range(1, k_width):
        y = y + xp[:, k:k + s] * w[k]
    return y


def layer_norm(x, g, b):
    xf = x.astype(jnp.float32)
    mu = jnp.mean(xf, axis=-1, keepdims=True)
    var = jnp.mean(jnp.square(xf - mu), axis=-1, keepdims=True)
    y = (xf - mu) * lax.rsqrt(var + LN_EPS)
    return y.astype(x.dtype) * g + b


def short_conv_mixer(x, w_in, conv_w, conv_b, w_out):
    h = jnp.einsum('bsd,de->bse', x, w_in)
    gate_b, gate_c, v = jnp.split(h, 3, axis=-1)
    u = causal_dwconv(gate_c * v, conv_w, conv_b)
    return jnp.einsum('bsd,de->bse', gate_b * u, w_out)


def _lru_combine(left, right):
    a_l, b_l = left
    a_r, b_r = right
    return a_l * a_r, a_r * b_l + b_r


def rglru_block(x, w_in, b_in, conv_w, conv_b, w_gate, b_gate, lam, w_out):
    bsz, s, _ = x.shape
    h = jnp.einsum('bsd,de->bse', x, w_in) + b_in
    g_branch, r_branch = jnp.split(h, 2, axis=-1)
    xr = causal_dwconv(r_branch, conv_w, conv_b)
    xh = xr.reshape(bsz, s, LRU_HEADS, LRU_BLOCK)
    gates = jnp.einsum('bshi,hio->bsho', xh, w_gate) + b_gate
    r_gate, i_gate = jnp.split(gates.astype(jnp.float32), 2, axis=-1)
    r_gate = jax.nn.sigmoid(r_gate).reshape(bsz, s, LRU_WIDTH)
    i_gate = jax.nn.sigmoid(i_gate).reshape(bsz, s, LRU_WIDTH)
    log_a = -LRU_C * r_gate * jax.nn.softplus(-lam.astype(jnp.float32))
    a = jnp.exp(log_a)
    mult = jnp.sqrt(-jnp.expm1(2.0 * log_a))
    b = mult * (i_gate * xr.astype(jnp.float32))
    _, hs = lax.associative_scan(_lru_combine, (a, b), axis=1)
    y = hs.astype(x.dtype) * jax.nn.gelu(g_branch, approximate=True)
    return jnp.einsum('bsr,rd->bsd', y, w_out)


def conv_ffn(x, w_up, conv_w, conv_b, w_down):
    h = jnp.einsum('bsd,df->bsf', x, w_up)
    h = causal_dwconv(h, conv_w, conv_b)
    g, v = jnp.split(h, 2, axis=-1)
    return jnp.einsum('bsf,fd->bsd', jax.nn.silu(g) * v, w_down)


def setup_inputs(seed: int = 0) -> dict:
    key = jax.random.key(seed)
    ks = jax.random.split(key, 24)
    d, r, f = D_MODEL, LRU_WIDTH, FFN_HIDDEN
    nA, nB, L = N_CONV_LAYERS, N_LRU_LAYERS, DEPTH
    nrm = jax.random.normal
    x = nrm(ks[0], (BATCH, SEQ, d), jnp.float32)
    sc_w_in = nrm(ks[1], (nA, d, 3 * d), jnp.float32) * d ** -0.5
    sc_conv_w = nrm(ks[2], (nA, SC_WIDTH, d), jnp.float32) * SC_WIDTH ** -0.5
    sc_conv_b = nrm(ks[3], (nA, d), jnp.float32) * 0.01
    sc_w_out = nrm(ks[4], (nA, d, d), jnp.float32) * d ** -0.5 * DEEPNORM_BETA
    lru_w_in = nrm(ks[5], (nB, d, 2 * r), jnp.float32) * d ** -0.5
    lru_b_in = nrm(ks[6], (nB, 2 * r), jnp.float32) * 0.01
    lru_conv_w = nrm(ks[7], (nB, LRU_CONV_WIDTH, r), jnp.float32) * LRU_CONV_WIDTH ** -0.5
    lru_conv_b = nrm(ks[8], (nB, r), jnp.float32) * 0.01
    lru_w_gate = nrm(ks[9], (nB, LRU_HEADS, LRU_BLOCK, 2 * LRU_BLOCK), jnp.float32) * LRU_BLOCK ** -0.5
    lru_b_gate = nrm(ks[10], (nB, LRU_HEADS, 2 * LRU_BLOCK), jnp.float32) * 0.01
    u = jax.random.uniform(ks[11], (nB, r), jnp.float32, 0.9, 0.999)
    p = u ** (1.0 / LRU_C)
    lru_lambda = jnp.log(p) - jnp.log1p(-p)
    lru_w_out = nrm(ks[12], (nB, r, d), jnp.float32) * r ** -0.5 * DEEPNORM_BETA
    ffn_w_up = nrm(ks[13], (L, d, 2 * f), jnp.float32) * d ** -0.5
    ffn_conv_w = nrm(ks[14], (L, FFN_CONV_WIDTH, 2 * f), jnp.float32) * FFN_CONV_WIDTH ** -0.5
    ffn_conv_b = nrm(ks[15], (L, 2 * f), jnp.float32) * 0.01
    ffn_w_down = nrm(ks[16], (L, f, d), jnp.float32) * f ** -0.5 * DEEPNORM_BETA
    ln_g = 1.0 + 0.02 * nrm(ks[17], (L, 2, d), jnp.float32)
    ln_b = 0.02 * nrm(ks[18], (L, 2, d), jnp.float32)
    return {"x": x,
            "sc_w_in": sc_w_in, "sc_conv_w": sc_conv_w, "sc_conv_b": sc_conv_b, "sc_w_out": sc_w_out,
            "lru_w_in": lru_w_in, "lru_b_in": lru_b_in, "lru_conv_w": lru_conv_w, "lru_conv_b": lru_conv_b,
            "lru_w_gate": lru_w_gate, "lru_b_gate": lru_b_gate, "lru_lambda": lru_lambda, "lru_w_out": lru_w_out,
            "ffn_w_up": ffn_w_up, "ffn_conv_w": ffn_conv_w, "ffn_conv_b": ffn_conv_b, "ffn_w_down": ffn_w_down,
            "ln_g": ln_g, "ln_b": ln_b}


def reference(x, sc_w_in, sc_conv_w, sc_conv_b, sc_w_out,
              lru_w_in, lru_b_in, lru_conv_w, lru_conv_b, lru_w_gate, lru_b_gate, lru_lambda, lru_w_out,
              ffn_w_up, ffn_conv_w, ffn_conv_b, ffn_w_down, ln_g, ln_b):
    for i in range(DEPTH):
        j = i // N_MIXERS
        if i % N_MIXERS == 0:
            y = short_conv_mixer(x, sc_w_in[j], sc_conv_w[j], sc_conv_b[j], sc_w_out[j])
        else:
            y = rglru_block(x, lru_w_in[j], lru_b_in[j], lru_conv_w[j], lru_conv_b[j],
                            lru_w_gate[j], lru_b_gate[j], lru_lambda[j], lru_w_out[j])
        x = layer_norm(DEEPNORM_ALPHA * x + y, ln_g[i, 0], ln_b[i, 0])
        y = conv_ffn(x, ffn_w_up[i], ffn_conv_w[i], ffn_conv_b[i], ffn_w_down[i])
        x = layer_norm(DEEPNORM_ALPHA * x + y, ln_g[i, 1], ln_b[i, 1])
    return x
```

```python
import numpy as np
import concourse.bass as bass
import concourse.mybir as mybir
from concourse.bass_utils import run_bass_kernel_spmd

F32 = mybir.dt.float32
BF16 = mybir.dt.bfloat16
AF = mybir.ActivationFunctionType
ALU = mybir.AluOpType

D = 1024
T = 2048
NT = 4
TS = 512
DC = 8
RW = 1280
RC = 10
FF = 2816
FC = 22
DEPTH = 4
ALPHA = float((2.0 * DEPTH) ** 0.25)
EPS = 1e-5
NCORES = 8
ALL_ENG = ("pe", "act", "dve", "pool", "sp")


class Op:
    __slots__ = ("eng", "idx", "emit", "deps", "dma_sem", "dma_val", "signum",
                 "needs_signal", "clock", "waits")

    def __init__(self, eng, idx, emit, deps, dma_sem=None):
        self.eng = eng
        self.idx = idx
        self.emit = emit
        self.deps = deps
        self.dma_sem = dma_sem
        self.dma_val = None
        self.signum = None
        self.needs_signal = False
        self.clock = None
        self.waits = ()


class Sched:
    def __init__(self):
        self.ops = {e: [] for e in ALL_ENG}
        self.order = []
        self.last_writer = {}
        self.readers = {}
        self.dma_counts = {}

    @staticmethod
    def _expand(keys):
        out = []
        for k in keys:
            if len(k) == 2 and k[0] == "work":
                out.extend(("work", k[1], n) for n in range(NT))
            else:
                out.append(k)
        return out

    def op(self, eng, emit, reads=(), writes=(), dma_sem=None, dma_n=1):
        reads = self._expand(reads)
        writes = self._expand(writes)
        deps = []
        seen = set()
        for r in reads:
            w = self.last_writer.get(r)
            if w is not None and id(w) not in seen:
                seen.add(id(w)); deps.append(w)
        for r in writes:
            w = self.last_writer.get(r)
            if w is not None and id(w) not in seen:
                seen.add(id(w)); deps.append(w)
            for rd in self.readers.get(r, ()):
                if id(rd) not in seen:
                    seen.add(id(rd)); deps.append(rd)
        o = Op(eng, len(self.ops[eng]), emit, deps, dma_sem)
        if dma_sem is not None:
            c = self.dma_counts.get(dma_sem, 0) + 16 * dma_n
            self.dma_counts[dma_sem] = c
            o.dma_val = c
        self.ops[eng].append(o)
        self.order.append(o)
        for r in reads:
            self.readers.setdefault(r, []).append(o)
        for r in writes:
            self.last_writer[r] = o
            self.readers[r] = []
        return o

    def finalize(self):
        known = {e: {} for e in ALL_ENG}
        for o in self.order:
            K = known[o.eng]
            waits = {}
            for d in o.deps:
                if d.dma_sem is not None:
                    key, v = ("dma", d.dma_sem), d.dma_val
                else:
                    if d.eng == "pe" and o.eng == "pe" and o.dma_sem is None:
                        continue
                    key, v = ("eng", d.eng), d.idx
                if K.get(key, -1) >= v:
                    continue
                cur = waits.get(key)
                if cur is None or cur[0] < v:
                    waits[key] = (v, d)
            if len(waits) > 1:
                for key in list(waits.keys()):
                    v, d = waits[key]
                    for k2, (v2, d2) in waits.items():
                        if k2 == key or d2.clock is None:
                            continue
                        if d2.clock.get(key, -1) >= v:
                            del waits[key]
                            break
            wl = []
            for key, (v, d) in waits.items():
                K[key] = v
                if d.dma_sem is None:
                    d.needs_signal = True
                wl.append((key, d))
            o.waits = wl
            for d in o.deps:
                if d.clock is None:
                    continue
                for k, v in d.clock.items():
                    if K.get(k, -1) < v:
                        K[k] = v
            c = dict(K)
            if o.dma_sem is not None:
                c[("dma", o.dma_sem)] = o.dma_val
            else:
                c[("eng", o.eng)] = o.idx
            o.clock = c
        for e in ALL_ENG:
            n = 0
            for o in self.ops[e]:
                if o.dma_sem is None and o.needs_signal:
                    n += 1
                    o.signum = n
        for o in self.order:
            o.clock = None

    def emit_engine(self, eng, engobj, sems, dma_sems):
        for o in self.ops[eng]:
            for key, d in o.waits:
                if key[0] == "dma":
                    engobj.wait_ge(dma_sems[key[1]], d.dma_val)
                else:
                    engobj.wait_ge(sems[key[1]], d.signum)
            if o.emit is None:
                continue
            if o.dma_sem is not None:
                o.emit(engobj, dma_sems[o.dma_sem])
            else:
                ins = o.emit(engobj)
                if o.needs_signal:
                    ins.then_inc(sems[eng], 1)


def _cols(vec):
    v = np.asarray(vec, np.float32).reshape(-1, 128)
    return np.ascontiguousarray(v.T)


def _wtiles(W):
    K, M = W.shape
    return np.ascontiguousarray(W.reshape(K // 128, 128, M // 128, 128).transpose(2, 1, 0, 3))


class ParamLayout:
    def __init__(self):
        self.off = {}
        self.n = 0
        self.blocks = []

    def add(self, name, arr=None, width=None):
        if arr is not None:
            width = arr.shape[1]
        self.off[name] = (self.n, width)
        self.blocks.append((self.n, arr))
        self.n += width

    def build(self):
        out = np.zeros((128, self.n), np.float32)
        for o, a in self.blocks:
            if a is not None:
                out[:, o:o + a.shape[1]] = a
        return out


def build_params(inp, layers):
    PL = ParamLayout()
    for i in layers:
        j = i // 2
        for s in range(2):
            PL.add(f"lng{i}_{s}", _cols(inp["ln_g"][i, s]))
            PL.add(f"lnb{i}_{s}", _cols(inp["ln_b"][i, s]))
        for k in range(3):
            PL.add(f"fcw{i}_{k}", _cols(inp["ffn_conv_w"][i, k]))
        PL.add(f"fcb{i}", _cols(inp["ffn_conv_b"][i]))
        if i % 2 == 0:
            for k in range(3):
                PL.add(f"scw{i}_{k}", _cols(inp["sc_conv_w"][j, k]))
            PL.add(f"scb{i}", _cols(inp["sc_conv_b"][j]))
        else:
            PL.add(f"lbin{i}", _cols(inp["lru_b_in"][j]))
            for k in range(4):
                PL.add(f"lcw{i}_{k}", _cols(inp["lru_conv_w"][j, k]))
            PL.add(f"lcb{i}", _cols(inp["lru_conv_b"][j]))
            bg = inp["lru_b_gate"][j]
            PL.add(f"lbg{i}", np.concatenate([_cols(bg[:, :128].reshape(-1)),
                                               _cols(bg[:, 128:].reshape(-1))], axis=1))
            PL.add(f"llam{i}", _cols(inp["lru_lambda"][j]))
            PL.add(f"lc{i}", width=RC)
            PL.add(f"lhc{i}", width=RC)
            PL.add(f"lhbg{i}", width=2 * RC)
            PL.add(f"ltmp{i}", width=RC)
    return PL


def build_nc(nseq, layers, PL):
    nc = bass.Bass("TRN2", target_bir_lowering=False)
    NP = PL.n
    xin = nc.dram_tensor("xin", [nseq, 128, DC, T], F32, kind="ExternalInput").ap()
    par = nc.dram_tensor("par", [128, NP], F32, kind="ExternalInput").ap()
    yout = nc.dram_tensor("yout", [nseq, 128, DC, T], F32, kind="ExternalOutput").ap()
    wd = {}
    for i in layers:
        if i % 2 == 0:
            wd[f"win{i}"] = nc.dram_tensor(f"win{i}", [24, 128, 8, 128], F32, kind="ExternalInput").ap()
            wd[f"wout{i}"] = nc.dram_tensor(f"wout{i}", [8, 128, 8, 128], F32, kind="ExternalInput").ap()
        else:
            wd[f"win{i}"] = nc.dram_tensor(f"win{i}", [20, 128, 8, 128], F32, kind="ExternalInput").ap()
            wd[f"wg{i}"] = nc.dram_tensor(f"wg{i}", [10, 128, 2, 128], F32, kind="ExternalInput").ap()
            wd[f"wout{i}"] = nc.dram_tensor(f"wout{i}", [8, 128, 10, 128], F32, kind="ExternalInput").ap()
        wd[f"wup{i}"] = nc.dram_tensor(f"wup{i}", [44, 128, 8, 128], F32, kind="ExternalInput").ap()
        wd[f"wdn{i}"] = nc.dram_tensor(f"wdn{i}", [8, 128, 22, 128], F32, kind="ExternalInput").ap()

    S = Sched()
    NWB = 4
    NWS = 2
    NWORK = 5
    WPAD = 4
    from contextlib import ExitStack
    with ExitStack() as es:
        x32 = es.enter_context(nc.sbuf_tensor("x32", [128, DC, T], F32))
        xbf = es.enter_context(nc.sbuf_tensor("xbf", [128, DC, T], BF16))
        act = es.enter_context(nc.sbuf_tensor("act", [128, 8, T], BF16))
        work = [es.enter_context(nc.sbuf_tensor(f"work{i}", [128, 4 * (WPAD + T // 4)], F32)) for i in range(NWORK)]
        xrbf = es.enter_context(nc.sbuf_tensor("xrbf", [128, T], BF16))
        HH = T // 2
        wst = [es.enter_context(nc.sbuf_tensor(f"wst{i}", [128, 8, 128], F32)) for i in range(NWS)]
        wbf = [es.enter_context(nc.sbuf_tensor(f"wbf{i}", [128, 8, 128], BF16)) for i in range(NWB)]
        prm = es.enter_context(nc.sbuf_tensor("prm", [128, NP], F32))
        cst = es.enter_context(nc.sbuf_tensor("cst", [128, TS], F32))
        ones = es.enter_context(nc.sbuf_tensor("ones", [128, 128], BF16))
        kc = es.enter_context(nc.sbuf_tensor("kc", [128, 8], F32))
        ps = es.enter_context(nc.psum_tensor("ps", [128, 8, TS], F32))
        sems = {e: es.enter_context(nc.semaphore(f"s_{e}")) for e in ("pe", "act", "dve", "pool")}
        dsem_names = [f"wst{i}" for i in range(NWS)] + [f"xin{c}" for c in range(DC)] + ["par", "out0", "out1", "out2"]
        dsems = {n: es.enter_context(nc.semaphore(f"d_{n}")) for n in dsem_names}
        block = es.enter_context(nc.Block())

        psflat = ps[:].rearrange("p b n -> p (b n)")

        def pcol(name, c=0, w=1):
            o, _ = PL.off[name]
            return prm[:, o + c:o + c + w]

        RPAR = [("par",)]

        def sub(n):
            return slice(n * TS, (n + 1) * TS)

        def setup_ops():
            S.op("sp", lambda e, sm: e.dma_start(out=prm[:], in_=par[:, :]).then_inc(sm, 16), writes=RPAR, dma_sem="par")
            S.op("pool", lambda e: e.memset(cst[:], -0.5), writes=[("cst",)])
            S.op("pool", lambda e: e.memset(ones[:], 1.0), writes=[("cst",)])
            S.op("pool", lambda e: e.memset(kc[:, 0:1], 0.0), writes=[("cst",)])
            S.op("pool", lambda e: e.memset(kc[:, 1:2], 1.0), writes=[("cst",)])
            S.op("pool", lambda e: e.memset(kc[:, 2:3], EPS), writes=[("cst",)])
            for wi in range(NWORK):
                S.op("pool", (lambda wi=wi: (lambda e: e.memset(work[wi][:, 0:WPAD], 0.0)))(),
                     writes=[("work", wi)])
            for i in layers:
                if i % 2 == 1:
                    lam = pcol(f"llam{i}", 0, RC)
                    tmp = pcol(f"ltmp{i}", 0, RC)
                    cc = pcol(f"lc{i}", 0, RC)
                    hc = pcol(f"lhc{i}", 0, RC)
                    S.op("act", (lambda tmp=tmp, lam=lam: (lambda e: e.activation(out=tmp, in_=lam, func=AF.Exp, scale=-1.0)))(),
                         reads=RPAR, writes=RPAR)
                    S.op("act", (lambda tmp=tmp: (lambda e: e.activation(out=tmp, in_=tmp, func=AF.Ln, bias=kc[:, 1:2], scale=1.0)))(),
                         reads=RPAR + [("cst",)], writes=RPAR)
                    S.op("dve", (lambda tmp=tmp, cc=cc: (lambda e: e.tensor_scalar(out=cc, in0=tmp, scalar1=-8.0, scalar2=None, op0=ALU.mult)))(),
                         reads=RPAR, writes=RPAR)
                    S.op("dve", (lambda tmp=tmp, hc=hc: (lambda e: e.tensor_scalar(out=hc, in0=tmp, scalar1=-4.0, scalar2=None, op0=ALU.mult)))(),
                         reads=RPAR, writes=RPAR)
                    S.op("dve", (lambda i=i: (lambda e: e.tensor_scalar(out=pcol(f"lhbg{i}", 0, 2 * RC), in0=pcol(f"lbg{i}", 0, 2 * RC),
                                                                       scalar1=0.5, scalar2=None, op0=ALU.mult)))(),
                         reads=RPAR, writes=RPAR)

        wctr = [0]
        wissued = [0]
        wlist = []
        collect = [True]
        LOOKAHEAD = 3

        def issue_w(k):
            dram_ap, kcn = wlist[k]
            ss = k % NWS
            bs = k % NWB
            S.op("sp", (lambda ss=ss, dram_ap=dram_ap, kcn=kcn: (lambda e, sm: e.dma_start(out=wst[ss][:, 0:kcn, :], in_=dram_ap).then_inc(sm, 16)))(),
                 writes=[("wst", ss)], dma_sem=f"wst{ss}")
            S.op("act", (lambda ss=ss, bs=bs, kcn=kcn: (lambda e: e.activation(out=wbf[bs][:, 0:kcn, :], in_=wst[ss][:, 0:kcn, :], func=AF.Copy)))(),
                 reads=[("wst", ss)], writes=[("wbf", bs)])

        def load_w(dram_ap, kcn, la=LOOKAHEAD):
            k = wctr[0]
            wctr[0] += 1
            if collect[0]:
                wlist.append((dram_ap, kcn))
                return k % NWB
            while wissued[0] <= min(k + la, len(wlist) - 1):
                issue_w(wissued[0])
                wissued[0] += 1
            return k % NWB

        bctr = [0]

        def next_bank():
            b = bctr[0] % 8
            bctr[0] += 1
            return b

        def next_pair():
            if bctr[0] % 2:
                bctr[0] += 1
            b = bctr[0] % 8
            bctr[0] += 2
            return b

        def mm_group(bank, bs, rhs_list, rhs_res):
            nk = len(rhs_list)

            def emit(e):
                ins = None
                for k in range(nk):
                    ins = e.matmul(ps[:, bank, :], wbf[bs][:, k, :], rhs_list[k], start=(k == 0), stop=(k == nk - 1))
                return ins
            S.op("pe", emit, reads=[("wbf", bs)] + list(rhs_res), writes=[("ps", bank)])

        def xbf_rhs(n):
            return [xbf[:, c, sub(n)] for c in range(DC)], [("xbf", c, n) for c in range(DC)]

        def down_proj(wname, k0, k1, first):
            kcn = k1 - k0
            for m in range(DC):
                bs = load_w(wd[wname][m, :, k0:k1, :], kcn)
                for q in range(NT // 2):
                    bank = next_pair()
                    for r in range(2):
                        n = 2 * q + r
                        mm_group(bank + r, bs, [act[:, kk, sub(n)] for kk in range(kcn)],
                                 [("act", kk, n) for kk in range(kcn)])
                    xs = x32[:, m, 2 * q * TS:(2 * q + 2) * TS]
                    pin = psflat[:, bank * TS:(bank + 2) * TS]
                    rk = [("ps", bank), ("ps", bank + 1), ("x32", m, 2 * q), ("x32", m, 2 * q + 1)]
                    wk = [("x32", m, 2 * q), ("x32", m, 2 * q + 1)]
                    if first:
                        S.op("dve", (lambda xs=xs, pin=pin: (lambda e: e.scalar_tensor_tensor(
                            out=xs, in0=xs, scalar=ALPHA, in1=pin, op0=ALU.mult, op1=ALU.add)))(),
                            reads=rk, writes=wk)
                    else:
                        S.op("dve", (lambda xs=xs, pin=pin: (lambda e: e.tensor_tensor(
                            out=xs, in0=xs, in1=pin, op=ALU.add)))(),
                            reads=rk, writes=wk)

        def mm_kouter(items):
            for k in range(DC):
                def emit(e, k=k):
                    ins = None
                    for (bank, bs, n) in items:
                        ins = e.matmul(ps[:, bank, :], wbf[bs][:, k, :], xbf[:, k, sub(n)], start=(k == 0), stop=(k == DC - 1))
                    return ins
                S.op("pe", emit, reads=[("wbf", bs) for (_, bs, _) in items] + [("xbf", k, n) for (_, _, n) in items],
                     writes=[("ps", bank) for (bank, _, _) in items])

        def layer_norm(gname, bname, out_seq=None):
            W0, W1, WM, WR = 0, 1, 3, 4
            for c in range(DC):
                wb = work[c % 2]
                rb = wb[:, WPAD:WPAD + T // 2].bitcast(BF16)
                sq = wb[:, WPAD + T // 2:WPAD + T].bitcast(BF16)
                S.op("act", (lambda rb=rb, c=c: (lambda e: e.activation(out=rb, in_=x32[:, c, :], func=AF.Copy)))(),
                     reads=[("x32", c, n) for n in range(NT)], writes=[("work", c % 2)])
                S.op("dve", (lambda sq=sq, c=c: (lambda e: e.tensor_tensor(out=sq, in0=x32[:, c, :], in1=x32[:, c, :], op=ALU.mult)))(),
                     reads=[("x32", c, n) for n in range(NT)], writes=[("work", c % 2)])

                def emit(e, rb=rb, sq=sq, c=c):
                    ins = None
                    for n in range(NT):
                        e.matmul(ps[:, n, :], ones[:], rb[:, sub(n)], start=(c == 0), stop=(c == DC - 1))
                        ins = e.matmul(ps[:, 4 + n, :], ones[:], sq[:, sub(n)], start=(c == 0), stop=(c == DC - 1))
                    return ins
                S.op("pe", emit, reads=[("work", c % 2), ("cst",)], writes=[("ps", b) for b in range(8)])
            bctr[0] = 0
            mean = work[WM][:, WPAD:WPAD + T]
            rstd = work[WR][:, WPAD:WPAD + T]
            psk_s = [("ps", n) for n in range(4)]
            psk_q = [("ps", 4 + n) for n in range(4)]
            S.op("act", lambda e: e.activation(out=mean, in_=psflat[:, 0:T], func=AF.Copy, scale=1.0 / D),
                 reads=psk_s, writes=[("work", WM)])
            S.op("act", lambda e: e.activation(out=rstd, in_=psflat[:, 0:T], func=AF.Square, scale=1.0 / D),
                 reads=psk_s, writes=[("work", WR)])
            S.op("dve", lambda e: e.scalar_tensor_tensor(out=rstd, in0=psflat[:, T:2 * T], scalar=1.0 / D, in1=rstd,
                                                         op0=ALU.mult, op1=ALU.subtract),
                 reads=psk_q + [("work", WR)], writes=[("work", WR)])
            S.op("act", lambda e: e.activation(out=rstd, in_=rstd, func=AF.Ln, bias=kc[:, 2:3], scale=1.0),
                 reads=[("work", WR), ("cst",)], writes=[("work", WR)])
            S.op("act", lambda e: e.activation(out=rstd, in_=rstd, func=AF.Exp, scale=-0.5),
                 reads=[("work", WR)], writes=[("work", WR)])
            for c in range(DC):
                xr_ = [("x32", c, n) for n in range(NT)]
                eng = "dve"
                S.op(eng, (lambda c=c: (lambda e: e.tensor_tensor(out=x32[:, c, :], in0=x32[:, c, :], in1=mean, op=ALU.subtract)))(),
                     reads=xr_ + [("work", WM)], writes=xr_)
                S.op(eng, (lambda c=c: (lambda e: e.tensor_tensor(out=x32[:, c, :], in0=x32[:, c, :], in1=rstd, op=ALU.mult)))(),
                     reads=xr_ + [("work", WR)], writes=xr_)
                if out_seq is None:
                    S.op("act", (lambda c=c: (lambda e: e.activation(out=x32[:, c, :], in_=x32[:, c, :], func=AF.Identity,
                                                                  bias=pcol(bname, c), scale=pcol(gname, c))))(),
                         reads=xr_ + RPAR, writes=xr_)
                    S.op("act", (lambda c=c: (lambda e: e.activation(out=xbf[:, c, :], in_=x32[:, c, :], func=AF.Copy)))(),
                         reads=xr_, writes=[("xbf", c, n) for n in range(NT)])
                else:
                    ob = c % 3
                    S.op("act", (lambda c=c, ob=ob: (lambda e: e.activation(out=work[ob][:, WPAD:WPAD + T], in_=x32[:, c, :], func=AF.Identity,
                                                                         bias=pcol(bname, c), scale=pcol(gname, c))))(),
                         reads=xr_ + RPAR, writes=[("work", ob)])
                    S.op("sp", (lambda c=c, ob=ob: (lambda e, sm: e.dma_start(out=yout[out_seq, :, c, :], in_=work[ob][:, WPAD:WPAD + T]).then_inc(sm, 16)))(),
                         reads=[("work", ob)], writes=[("yout", ob)], dma_sem=f"out{ob}")

        def sc_mixer(i):
            WV, WB, WC, WU = 0, 1, 2, 3
            for j in range(DC):
                pre = {}
                if j == 0:
                    bctr[0] = 0
                    items = []
                    for ti, (mchunk, dst) in enumerate(((16 + j, WV), (j, WB))):
                        bs_ = load_w(wd[f"win{i}"][mchunk, :, :, :], 8, LOOKAHEAD - ti)
                        for n in range(NT):
                            bank_ = next_bank()
                            pre[(dst, n)] = bank_
                            items.append((bank_, bs_, n))
                    mm_kouter(items)
                for (mchunk, dst) in ((16 + j, WV), (j, WB)):
                    if j > 0:
                        bs = load_w(wd[f"win{i}"][mchunk, :, :, :], 8)
                    for n in range(NT):
                        if j == 0:
                            bank = pre[(dst, n)]
                        else:
                            bank = next_bank()
                            rl, rr = xbf_rhs(n)
                            mm_group(bank, bs, rl, rr)
                        S.op("act", (lambda dst=dst, n=n, bank=bank: (lambda e: e.activation(
                            out=work[dst][:, WPAD + n * TS:WPAD + (n + 1) * TS], in_=ps[:, bank, :], func=AF.Copy)))(),
                            reads=[("ps", bank)], writes=[("work", dst)])
                bs = load_w(wd[f"win{i}"][8 + j, :, :, :], 8)
                for n in range(NT):
                    bank = next_bank()
                    rl, rr = xbf_rhs(n)
                    mm_group(bank, bs, rl, rr)
                    S.op("dve", (lambda n=n, bank=bank: (lambda e: e.tensor_tensor(
                        out=work[WC][:, WPAD + n * TS:WPAD + (n + 1) * TS], in0=ps[:, bank, :],
                        in1=work[WV][:, WPAD + n * TS:WPAD + (n + 1) * TS], op=ALU.mult)))(),
                        reads=[("ps", bank), ("work", WV)], writes=[("work", WC)])
                cv = work[WC]
                u = work[WU]
                S.op("dve", (lambda j=j: (lambda e: e.tensor_scalar(
                    out=u[:, WPAD:WPAD + T], in0=cv[:, WPAD:WPAD + T], scalar1=pcol(f"scw{i}_2", j), scalar2=pcol(f"scb{i}", j),
                    op0=ALU.mult, op1=ALU.add)))(),
                    reads=[("work", WC)] + RPAR, writes=[("work", WU)])
                for sh, kk in ((1, 1), (2, 0)):
                    S.op("dve", (lambda j=j, sh=sh, kk=kk: (lambda e: e.scalar_tensor_tensor(
                        out=u[:, WPAD:WPAD + T], in0=cv[:, WPAD - sh:WPAD - sh + T], scalar=pcol(f"scw{i}_{kk}", j),
                        in1=u[:, WPAD:WPAD + T], op0=ALU.mult, op1=ALU.add)))(),
                        reads=[("work", WC), ("work", WU)] + RPAR, writes=[("work", WU)])
                S.op("dve", (lambda j=j: (lambda e: e.tensor_tensor(
                    out=act[:, j, :], in0=u[:, WPAD:WPAD + T], in1=work[WB][:, WPAD:WPAD + T], op=ALU.mult)))(),
                    reads=[("work", WU), ("work", WB)], writes=[("act", j, n) for n in range(NT)])
            down_proj(f"wout{i}", 0, 8, True)

        def lru_mixer(i):
            NU = 4
            HH = T // NU
            HB = WPAD + HH

            def hw(sx, b):
                return work[b][:, sx * HB:(sx + 1) * HB]

            def Dv(sx, b):
                return work[b][:, sx * HB + WPAD:sx * HB + WPAD + HH]

            def K(sx, b):
                return [("work", b, sx)]

            XK = [[("xrbf", u)] for u in range(NU)]
            for grp, (h0, h1) in enumerate(((0, 5), (5, 10))):
                for h in range(h0, h1):
                    pre = {}
                    if h == 0:
                        bctr[0] = 0
                        items = []
                        for ti, (mchunk, dstb) in enumerate(((RC + h, 0), (h, 4))):
                            bs_ = load_w(wd[f"win{i}"][mchunk, :, :, :], 8, LOOKAHEAD - ti)
                            for sx in range(NU):
                                bank_ = next_bank()
                                pre[(dstb, sx)] = bank_
                                items.append((bank_, bs_, sx))
                        mm_kouter(items)
                    for (mchunk, dstb, bcol, fn) in ((RC + h, 0, RC + h, AF.Identity), (h, 4, h, AF.Gelu_apprx_tanh)):
                        if h > 0:
                            bs = load_w(wd[f"win{i}"][mchunk, :, :, :], 8)
                        for sx in range(NU):
                            if h == 0:
                                bank = pre[(dstb, sx)]
                            else:
                                bank = next_bank()
                                rl, rr = xbf_rhs(sx)
                                mm_group(bank, bs, rl, rr)
                            S.op("act", (lambda sx=sx, bank=bank, dstb=dstb, bcol=bcol, fn=fn: (lambda e: e.activation(
                                out=Dv(sx, dstb), in_=ps[:, bank, :], func=fn,
                                bias=pcol(f"lbin{i}", bcol), scale=1.0)))(),
                                reads=[("ps", bank)] + RPAR, writes=K(sx, dstb))
                    for sx in range(1, NU):
                        S.op("dve", (lambda sx=sx: (lambda e: e.tensor_copy(out=hw(sx, 0)[:, WPAD - 3:WPAD], in_=hw(sx - 1, 0)[:, HB - 3:HB])))(),
                             reads=K(sx - 1, 0), writes=K(sx, 0))
                    bsg = load_w(wd[f"wg{i}"][h, :, :, :], 2)
                    for sx in range(NU):
                        S.op("dve", (lambda sx=sx, h=h: (lambda e: e.tensor_scalar(
                            out=Dv(sx, 1), in0=Dv(sx, 0), scalar1=pcol(f"lcw{i}_3", h), scalar2=pcol(f"lcb{i}", h),
                            op0=ALU.mult, op1=ALU.add)))(),
                            reads=K(sx, 0) + RPAR, writes=K(sx, 1))
                        for sh, kk in ((1, 2), (2, 1), (3, 0)):
                            S.op("dve", (lambda sx=sx, h=h, sh=sh, kk=kk: (lambda e: e.scalar_tensor_tensor(
                                out=Dv(sx, 1), in0=hw(sx, 0)[:, WPAD - sh:WPAD - sh + HH], scalar=pcol(f"lcw{i}_{kk}", h),
                                in1=Dv(sx, 1), op0=ALU.mult, op1=ALU.add)))(),
                                reads=K(sx, 0) + K(sx, 1) + RPAR, writes=K(sx, 1))
                        S.op("act", (lambda sx=sx: (lambda e: e.activation(out=xrbf[:, sx * HH:(sx + 1) * HH], in_=Dv(sx, 1), func=AF.Copy)))(),
                             reads=K(sx, 1), writes=XK[sx])
                        for g in range(2):
                            bank = next_bank()

                            def emit(e, bank=bank, bsg=bsg, g=g, sx=sx):
                                return e.matmul(ps[:, bank, :], wbf[bsg][:, g, :], xrbf[:, sx * HH:(sx + 1) * HH], start=True, stop=True)
                            S.op("pe", emit, reads=[("wbf", bsg)] + XK[sx], writes=[("ps", bank)])
                            dstb = 2 if g == 0 else 0
                            S.op("act", (lambda h=h, g=g, sx=sx, bank=bank, dstb=dstb: (lambda e: e.activation(
                                out=Dv(sx, dstb), in_=ps[:, bank, :], func=AF.Tanh,
                                bias=pcol(f"lhbg{i}", g * RC + h), scale=0.5)))(),
                                reads=[("ps", bank)] + RPAR, writes=K(sx, dstb))
                        S.op("act", (lambda sx=sx, h=h: (lambda e: e.activation(out=Dv(sx, 3), in_=Dv(sx, 2), func=AF.Exp,
                                                                            bias=pcol(f"lhc{i}", h), scale=pcol(f"lhc{i}", h))))(),
                             reads=K(sx, 2) + RPAR, writes=K(sx, 3))
                        S.op("act", (lambda sx=sx, h=h: (lambda e: e.activation(out=Dv(sx, 2), in_=Dv(sx, 2), func=AF.Exp,
                                                                            bias=pcol(f"lc{i}", h), scale=pcol(f"lc{i}", h))))(),
                             reads=K(sx, 2) + RPAR, writes=K(sx, 2))
                    for sx in range(NU):
                        S.op("dve", (lambda sx=sx: (lambda e: e.tensor_scalar(out=Dv(sx, 2), in0=Dv(sx, 2),
                                                                           scalar1=0.9999999, scalar2=-1.0, op0=ALU.min, op1=ALU.mult)))(),
                             reads=K(sx, 2), writes=K(sx, 2))
                    for sx in range(NU):
                        S.op("act", (lambda sx=sx: (lambda e: e.activation(out=Dv(sx, 2), in_=Dv(sx, 2), func=AF.Sqrt,
                                                                      bias=kc[:, 1:2], scale=1.0)))(),
                             reads=K(sx, 2) + [("cst",)], writes=K(sx, 2))
                    for sx in range(NU):
                        S.op("dve", (lambda sx=sx: (lambda e: e.scalar_tensor_tensor(
                            out=Dv(sx, 0), in0=Dv(sx, 0), scalar=1.0, in1=Dv(sx, 1), op0=ALU.add, op1=ALU.mult)))(),
                            reads=K(sx, 0) + K(sx, 1), writes=K(sx, 0))
                        S.op("dve", (lambda sx=sx: (lambda e: e.scalar_tensor_tensor(
                            out=Dv(sx, 0), in0=Dv(sx, 0), scalar=0.5, in1=Dv(sx, 2), op0=ALU.mult, op1=ALU.mult)))(),
                            reads=K(sx, 0) + K(sx, 2), writes=K(sx, 0))
                        if sx == 0:
                            S.op("dve", (lambda: (lambda e: e.tensor_tensor_scan(
                                out=Dv(0, 1), data0=Dv(0, 3), data1=Dv(0, 0), initial=0.0, op0=ALU.mult, op1=ALU.add)))(),
                                reads=K(0, 3) + K(0, 0), writes=K(0, 1))
                        else:
                            S.op("dve", (lambda sx=sx: (lambda e: e.tensor_tensor_scan(
                                out=Dv(sx, 1), data0=Dv(sx, 3), data1=Dv(sx, 0), initial=Dv(sx - 1, 1)[:, HH - 1:HH],
                                op0=ALU.mult, op1=ALU.add)))(),
                                reads=K(sx, 3) + K(sx, 0) + K(sx - 1, 1), writes=K(sx, 1))
                        S.op("dve", (lambda sx=sx, h=h, h0=h0: (lambda e: e.tensor_tensor(
                            out=act[:, h - h0, sx * HH:(sx + 1) * HH], in0=Dv(sx, 4), in1=Dv(sx, 1), op=ALU.mult)))(),
                            reads=K(sx, 4) + K(sx, 1), writes=[("act", h - h0, sx)])
                down_proj(f"wout{i}", h0, h1, grp == 0)

        def ffn(i):
            groups = ((0, 8), (8, 15), (15, 22))
            CG, CV = 0, 1
            for grp, (j0, j1) in enumerate(groups):
                for j in range(j0, j1):
                    par_ = j % 2
                    cg = work[2 * par_]
                    cvv = work[2 * par_ + 1]
                    if j == 0:
                        items = []
                        for ti, (mchunk, base) in enumerate(((j, 0), (FC + j, 4))):
                            bs_ = load_w(wd[f"wup{i}"][mchunk, :, :, :], 8, LOOKAHEAD - ti)
                            items += [(base + n, bs_, n) for n in range(NT)]
                        mm_kouter(items)
                    for (mchunk, dstb, base) in ((j, 2 * par_, 0), (FC + j, 2 * par_ + 1, 4)):
                        dst = work[dstb]
                        if j > 0:
                            bs = load_w(wd[f"wup{i}"][mchunk, :, :, :], 8)
                            for n in range(NT):
                                rl, rr = xbf_rhs(n)
                                mm_group(base + n, bs, rl, rr)
                        pk = [("ps", base + n) for n in range(NT)]
                        S.op("act", (lambda mchunk=mchunk, base=base, dst=dst: (lambda e: e.activation(
                            out=dst[:, WPAD:WPAD + T], in_=psflat[:, base * TS:base * TS + T], func=AF.Identity,
                            bias=pcol(f"fcb{i}", mchunk), scale=pcol(f"fcw{i}_2", mchunk))))(),
                            reads=pk + RPAR, writes=[("work", dstb)])
                        for sh, kk in ((1, 1), (2, 0)):
                            S.op("dve", (lambda mchunk=mchunk, kk=kk, sh=sh, base=base, dst=dst: (lambda e: e.scalar_tensor_tensor(
                                out=dst[:, WPAD + sh:WPAD + T], in0=psflat[:, base * TS:base * TS + T - sh],
                                scalar=pcol(f"fcw{i}_{kk}", mchunk), in1=dst[:, WPAD + sh:WPAD + T],
                                op0=ALU.mult, op1=ALU.add)))(),
                                reads=pk + [("work", dstb)] + RPAR, writes=[("work", dstb)])
                    S.op("act", (lambda cg=cg: (lambda e: e.activation(out=cg[:, WPAD:WPAD + T], in_=cg[:, WPAD:WPAD + T], func=AF.Silu)))(),
                         reads=[("work", 2 * par_, n) for n in range(NT)], writes=[("work", 2 * par_, n) for n in range(NT)])
                    S.op("dve", (lambda cg=cg, cvv=cvv, j=j, j0=j0: (lambda e: e.tensor_tensor(
                        out=act[:, j - j0, :], in0=cg[:, WPAD:WPAD + T], in1=cvv[:, WPAD:WPAD + T], op=ALU.mult)))(),
                        reads=[("work", 2 * par_, n) for n in range(NT)] + [("work", 2 * par_ + 1, n) for n in range(NT)],
                        writes=[("act", j - j0, n) for n in range(NT)])
                bctr[0] = 0
                down_proj(f"wdn{i}", j0, j1, grp == 0)

        def work_barrier(idxs):
            pass

        def main_ops():
            for s in range(nseq):
                for c in range(DC):
                    S.op("sp", (lambda s=s, c=c: (lambda e, sm: e.dma_start(out=x32[:, c, :], in_=xin[s, :, c, :]).then_inc(sm, 16)))(),
                         writes=[("x32", c, n) for n in range(NT)], dma_sem=f"xin{c}")
                for c in range(DC):
                    S.op("act", (lambda c=c: (lambda e: e.activation(out=xbf[:, c, :], in_=x32[:, c, :], func=AF.Copy)))(),
                         reads=[("x32", c, n) for n in range(NT)], writes=[("xbf", c, n) for n in range(NT)])
                for li, i in enumerate(layers):
                    if i % 2 == 0:
                        sc_mixer(i)
                    else:
                        lru_mixer(i)
                    layer_norm(f"lng{i}_0", f"lnb{i}_0")
                    ffn(i)
                    last = (li == len(layers) - 1)
                    layer_norm(f"lng{i}_1", f"lnb{i}_1", s if last else None)
            S.op("sp", None, reads=[("yout", 0), ("yout", 1), ("yout", 2)])
        collect[0] = True
        setup_ops()
        main_ops()
        collect[0] = False
        S = Sched()
        wctr[0] = 0
        wissued[0] = 0
        bctr[0] = 0
        setup_ops()
        main_ops()
        S.finalize()

        engmap = {"sp": block.sync, "pe": block.tensor, "act": block.scalar, "dve": block.vector, "pool": block.gpsimd}

        @block.sync
        def _(e):
            S.emit_engine("sp", e, sems, dsems)

        @block.tensor
        def _(e):
            S.emit_engine("pe", e, sems, dsems)

        @block.scalar
        def _(e):
            S.emit_engine("act", e, sems, dsems)

        @block.vector
        def _(e):
            S.emit_engine("dve", e, sems, dsems)

        @block.gpsimd
        def _(e):
            S.emit_engine("pool", e, sems, dsems)
    return nc


def prep_weights(inp, layers):
    w = {}
    for i in layers:
        j = i // 2
        if i % 2 == 0:
            w[f"win{i}"] = _wtiles(np.asarray(inp["sc_w_in"][j], np.float32))
            w[f"wout{i}"] = _wtiles(np.asarray(inp["sc_w_out"][j], np.float32))
        else:
            w[f"win{i}"] = _wtiles(np.asarray(inp["lru_w_in"][j], np.float32))
            w[f"wg{i}"] = np.ascontiguousarray(np.asarray(inp["lru_w_gate"][j], np.float32).reshape(10, 128, 2, 128))
            w[f"wout{i}"] = _wtiles(np.asarray(inp["lru_w_out"][j], np.float32))
        w[f"wup{i}"] = _wtiles(np.asarray(inp["ffn_w_up"][i], np.float32))
        w[f"wdn{i}"] = _wtiles(np.asarray(inp["ffn_w_down"][i], np.float32))
    return w


def run(inp, layers, nseq, ncores=NCORES, trace=False):
    x = np.asarray(inp["x"], np.float32)
    PL = build_params(inp, layers)
    par = PL.build()
    w = prep_weights(inp, layers)
    nc = build_nc(nseq, layers, PL)
    in_maps = []
    for c in range(ncores):
        xs = x[c * nseq:(c + 1) * nseq]
        xT = np.ascontiguousarray(xs.reshape(nseq, T, DC, 128).transpose(0, 3, 2, 1))
        m = {"xin": xT, "par": par}
        m.update(w)
        in_maps.append(m)
    res = run_bass_kernel_spmd(nc, in_maps, core_ids=list(range(ncores)), trace=trace)
    outs = []
    for c in range(ncores):
        y = res.results[c]["yout"]
        outs.append(np.ascontiguousarray(y.transpose(0, 3, 2, 1)).reshape(nseq, T, D))
    return np.concatenate(outs, axis=0), res


def kernel(**inputs):
    out, _ = run(inputs, list(range(DEPTH)), 4)
    return out.astype(np.float32)
```

```python
import numpy as np
import concourse.bass as bass
import concourse.mybir as mybir
from concourse.bass_utils import run_bass_kernel_spmd

F32 = mybir.dt.float32
BF16 = mybir.dt.bfloat16
AF = mybir.ActivationFunctionType
ALU = mybir.AluOpType

D = 1024
T = 2048
NT = 4
TS = 512
DC = 8
RW = 1280
RC = 10
FF = 2816
FC = 22
DEPTH = 4
ALPHA = float((2.0 * DEPTH) ** 0.25)
EPS = 1e-5
NCORES = 8
ALL_ENG = ("pe", "act", "dve", "pool", "sp")


class Op:
    __slots__ = ("eng", "idx", "emit", "deps", "dma_sem", "dma_val", "signum",
                 "needs_signal", "clock", "waits")

    def __init__(self, eng, idx, emit, deps, dma_sem=None):
        self.eng = eng
        self.idx = idx
        self.emit = emit
        self.deps = deps
        self.dma_sem = dma_sem
        self.dma_val = None
        self.signum = None
        self.needs_signal = False
        self.clock = None
        self.waits = ()


class Sched:
    def __init__(self):
        self.ops = {e: [] for e in ALL_ENG}
        self.order = []
        self.last_writer = {}
        self.readers = {}
        self.dma_counts = {}

    @staticmethod
    def _expand(keys):
        out = []
        for k in keys:
            if len(k) == 2 and k[0] == "work":
                out.extend(("work", k[1], n) for n in range(NT))
            else:
                out.append(k)
        return out

    def op(self, eng, emit, reads=(), writes=(), dma_sem=None, dma_n=1):
        reads = self._expand(reads)
        writes = self._expand(writes)
        deps = []
        seen = set()
        for r in reads:
            w = self.last_writer.get(r)
            if w is not None and id(w) not in seen:
                seen.add(id(w)); deps.append(w)
        for r in writes:
            w = self.last_writer.get(r)
            if w is not None and id(w) not in seen:
                seen.add(id(w)); deps.append(w)
            for rd in self.readers.get(r, ()):
                if id(rd) not in seen:
                    seen.add(id(rd)); deps.append(rd)
        o = Op(eng, len(self.ops[eng]), emit, deps, dma_sem)
        if dma_sem is not None:
            c = self.dma_counts.get(dma_sem, 0) + 16 * dma_n
            self.dma_counts[dma_sem] = c
            o.dma_val = c
        self.ops[eng].append(o)
        self.order.append(o)
        for r in reads:
            self.readers.setdefault(r, []).append(o)
        for r in writes:
            self.last_writer[r] = o
            self.readers[r] = []
        return o

    def finalize(self):
        known = {e: {} for e in ALL_ENG}
        for o in self.order:
            K = known[o.eng]
            waits = {}
            for d in o.deps:
                if d.dma_sem is not None:
                    key, v = ("dma", d.dma_sem), d.dma_val
                else:
                    if d.eng == "pe" and o.eng == "pe" and o.dma_sem is None:
                        continue
                    key, v = ("eng", d.eng), d.idx
                if K.get(key, -1) >= v:
                    continue
                cur = waits.get(key)
                if cur is None or cur[0] < v:
                    waits[key] = (v, d)
            if len(waits) > 1:
                for key in list(waits.keys()):
                    v, d = waits[key]
                    for k2, (v2, d2) in waits.items():
                        if k2 == key or d2.clock is None:
                            continue
                        if d2.clock.get(key, -1) >= v:
                            del waits[key]
                            break
            wl = []
            for key, (v, d) in waits.items():
                K[key] = v
                if d.dma_sem is None:
                    d.needs_signal = True
                wl.append((key, d))
            o.waits = wl
            for d in o.deps:
                if d.clock is None:
                    continue
                for k, v in d.clock.items():
                    if K.get(k, -1) < v:
                        K[k] = v
            c = dict(K)
            if o.dma_sem is not None:
                c[("dma", o.dma_sem)] = o.dma_val
            else:
                c[("eng", o.eng)] = o.idx
            o.clock = c
        for e in ALL_ENG:
            n = 0
            for o in self.ops[e]:
                if o.dma_sem is None and o.needs_signal:
                    n += 1
                    o.signum = n
        for o in self.order:
            o.clock = None

    def emit_engine(self, eng, engobj, sems, dma_sems):
        for o in self.ops[eng]:
            for key, d in o.waits:
                if key[0] == "dma":
                    engobj.wait_ge(dma_sems[key[1]], d.dma_val)
                else:
                    engobj.wait_ge(sems[key[1]], d.signum)
            if o.emit is None:
                continue
            if o.dma_sem is not None:
                o.emit(engobj, dma_sems[o.dma_sem])
            else:
                ins = o.emit(engobj)
                if o.needs_signal:
                    ins.then_inc(sems[eng], 1)


def _cols(vec):
    v = np.asarray(vec, np.float32).reshape(-1, 128)
    return np.ascontiguousarray(v.T)


def _wtiles(W):
    K, M = W.shape
    return np.ascontiguousarray(W.reshape(K // 128, 128, M // 128, 128).transpose(2, 1, 0, 3))


class ParamLayout:
    def __init__(self):
        self.off = {}
        self.n = 0
        self.blocks = []

    def add(self, name, arr=None, width=None):
        if arr is not None:
            width = arr.shape[1]
        self.off[name] = (self.n, width)
        self.blocks.append((self.n, arr))
        self.n += width

    def build(self):
        out = np.zeros((128, self.n), np.float32)
        for o, a in self.blocks:
            if a is not None:
                out[:, o:o + a.shape[1]] = a
        return out


def build_params(inp, layers):
    PL = ParamLayout()
    for i in layers:
        j = i // 2
        for s in range(2):
            PL.add(f"lng{i}_{s}", _cols(inp["ln_g"][i, s]))
            PL.add(f"lnb{i}_{s}", _cols(inp["ln_b"][i, s]))
        for k in range(3):
            PL.add(f"fcw{i}_{k}", _cols(inp["ffn_conv_w"][i, k]))
        PL.add(f"fcb{i}", _cols(inp["ffn_conv_b"][i]))
        if i % 2 == 0:
            for k in range(3):
                PL.add(f"scw{i}_{k}", _cols(inp["sc_conv_w"][j, k]))
            PL.add(f"scb{i}", _cols(inp["sc_conv_b"][j]))
        else:
            PL.add(f"lbin{i}", _cols(inp["lru_b_in"][j]))
            for k in range(4):
                PL.add(f"lcw{i}_{k}", _cols(inp["lru_conv_w"][j, k]))
            PL.add(f"lcb{i}", _cols(inp["lru_conv_b"][j]))
            bg = inp["lru_b_gate"][j]
            PL.add(f"lbg{i}", np.concatenate([_cols(bg[:, :128].reshape(-1)),
                                               _cols(bg[:, 128:].reshape(-1))], axis=1))
            PL.add(f"llam{i}", _cols(inp["lru_lambda"][j]))
            PL.add(f"lc{i}", width=RC)
            PL.add(f"lhc{i}", width=RC)
            PL.add(f"lhbg{i}", width=2 * RC)
            PL.add(f"ltmp{i}", width=RC)
    return PL


def build_nc(nseq, layers, PL):
    nc = bass.Bass("TRN2", target_bir_lowering=False)
    NP = PL.n
    xin = nc.dram_tensor("xin", [nseq, 128, DC, T], F32, kind="ExternalInput").ap()
    par = nc.dram_tensor("par", [128, NP], F32, kind="ExternalInput").ap()
    yout = nc.dram_tensor("yout", [nseq, 128, DC, T], F32, kind="ExternalOutput").ap()
    wd = {}
    for i in layers:
        if i % 2 == 0:
            wd[f"win{i}"] = nc.dram_tensor(f"win{i}", [24, 128, 8, 128], F32, kind="ExternalInput").ap()
            wd[f"wout{i}"] = nc.dram_tensor(f"wout{i}", [8, 128, 8, 128], F32, kind="ExternalInput").ap()
        else:
            wd[f"win{i}"] = nc.dram_tensor(f"win{i}", [20, 128, 8, 128], F32, kind="ExternalInput").ap()
            wd[f"wg{i}"] = nc.dram_tensor(f"wg{i}", [10, 128, 2, 128], F32, kind="ExternalInput").ap()
            wd[f"wout{i}"] = nc.dram_tensor(f"wout{i}", [8, 128, 10, 128], F32, kind="ExternalInput").ap()
        wd[f"wup{i}"] = nc.dram_tensor(f"wup{i}", [44, 128, 8, 128], F32, kind="ExternalInput").ap()
        wd[f"wdn{i}"] = nc.dram_tensor(f"wdn{i}", [8, 128, 22, 128], F32, kind="ExternalInput").ap()

    S = Sched()
    NWB = 4
    NWS = 2
    NWORK = 5
    WPAD = 4
    from contextlib import ExitStack
    with ExitStack() as es:
        x32 = es.enter_context(nc.sbuf_tensor("x32", [128, DC, T], F32))
        xbf = es.enter_context(nc.sbuf_tensor("xbf", [128, DC, T], BF16))
        act = es.enter_context(nc.sbuf_tensor("act", [128, 8, T], BF16))
        work = [es.enter_context(nc.sbuf_tensor(f"work{i}", [128, 4 * (WPAD + T // 4)], F32)) for i in range(NWORK)]
        xrbf = es.enter_context(nc.sbuf_tensor("xrbf", [128, T], BF16))
        HH = T // 2
        wst = [es.enter_context(nc.sbuf_tensor(f"wst{i}", [128, 8, 128], F32)) for i in range(NWS)]
        wbf = [es.enter_context(nc.sbuf_tensor(f"wbf{i}", [128, 8, 128], BF16)) for i in range(NWB)]
        prm = es.enter_context(nc.sbuf_tensor("prm", [128, NP], F32))
        cst = es.enter_context(nc.sbuf_tensor("cst", [128, TS], F32))
        ones = es.enter_context(nc.sbuf_tensor("ones", [128, 128], BF16))
        kc = es.enter_context(nc.sbuf_tensor("kc", [128, 8], F32))
        ps = es.enter_context(nc.psum_tensor("ps", [128, 8, TS], F32))
        sems = {e: es.enter_context(nc.semaphore(f"s_{e}")) for e in ("pe", "act", "dve", "pool")}
        dsem_names = [f"wst{i}" for i in range(NWS)] + [f"xin{c}" for c in range(DC)] + ["par", "out0", "out1", "out2"]
        dsems = {n: es.enter_context(nc.semaphore(f"d_{n}")) for n in dsem_names}
        block = es.enter_context(nc.Block())

        psflat = ps[:].rearrange("p b n -> p (b n)")

        def pcol(name, c=0, w=1):
            o, _ = PL.off[name]
            return prm[:, o + c:o + c + w]

        RPAR = [("par",)]

        def sub(n):
            return slice(n * TS, (n + 1) * TS)

        def setup_ops():
            S.op("sp", lambda e, sm: e.dma_start(out=prm[:], in_=par[:, :]).then_inc(sm, 16), writes=RPAR, dma_sem="par")
            S.op("pool", lambda e: e.memset(cst[:], -0.5), writes=[("cst",)])
            S.op("pool", lambda e: e.memset(ones[:], 1.0), writes=[("cst",)])
            S.op("pool", lambda e: e.memset(kc[:, 0:1], 0.0), writes=[("cst",)])
            S.op("pool", lambda e: e.memset(kc[:, 1:2], 1.0), writes=[("cst",)])
            S.op("pool", lambda e: e.memset(kc[:, 2:3], EPS), writes=[("cst",)])
            for wi in range(NWORK):
                S.op("pool", (lambda wi=wi: (lambda e: e.memset(work[wi][:, 0:WPAD], 0.0)))(),
                     writes=[("work", wi)])
            for i in layers:
                if i % 2 == 1:
                    lam = pcol(f"llam{i}", 0, RC)
                    tmp = pcol(f"ltmp{i}", 0, RC)
                    cc = pcol(f"lc{i}", 0, RC)
                    hc = pcol(f"lhc{i}", 0, RC)
                    S.op("act", (lambda tmp=tmp, lam=lam: (lambda e: e.activation(out=tmp, in_=lam, func=AF.Exp, scale=-1.0)))(),
                         reads=RPAR, writes=RPAR)
                    S.op("act", (lambda tmp=tmp: (lambda e: e.activation(out=tmp, in_=tmp, func=AF.Ln, bias=kc[:, 1:2], scale=1.0)))(),
                         reads=RPAR + [("cst",)], writes=RPAR)
                    S.op("dve", (lambda tmp=tmp, cc=cc: (lambda e: e.tensor_scalar(out=cc, in0=tmp, scalar1=-8.0, scalar2=None, op0=ALU.mult)))(),
                         reads=RPAR, writes=RPAR)
                    S.op("dve", (lambda tmp=tmp, hc=hc: (lambda e: e.tensor_scalar(out=hc, in0=tmp, scalar1=-4.0, scalar2=None, op0=ALU.mult)))(),
                         reads=RPAR, writes=RPAR)
                    S.op("dve", (lambda i=i: (lambda e: e.tensor_scalar(out=pcol(f"lhbg{i}", 0, 2 * RC), in0=pcol(f"lbg{i}", 0, 2 * RC),
                                                                       scalar1=0.5, scalar2=None, op0=ALU.mult)))(),
                         reads=RPAR, writes=RPAR)

        wctr = [0]
        wissued = [0]
        wlist = []
        collect = [True]
        LOOKAHEAD = 3

        def issue_w(k):
            dram_ap, kcn = wlist[k]
            ss = k % NWS
            bs = k % NWB
            S.op("sp", (lambda ss=ss, dram_ap=dram_ap, kcn=kcn: (lambda e, sm: e.dma_start(out=wst[ss][:, 0:kcn, :], in_=dram_ap).then_inc(sm, 16)))(),
                 writes=[("wst", ss)], dma_sem=f"wst{ss}")
            S.op("act", (lambda ss=ss, bs=bs, kcn=kcn: (lambda e: e.activation(out=wbf[bs][:, 0:kcn, :], in_=wst[ss][:, 0:kcn, :], func=AF.Copy)))(),
                 reads=[("wst", ss)], writes=[("wbf", bs)])

        def load_w(dram_ap, kcn, la=LOOKAHEAD):
            k = wctr[0]
            wctr[0] += 1
            if collect[0]:
                wlist.append((dram_ap, kcn))
                return k % NWB
            while wissued[0] <= min(k + la, len(wlist) - 1):
                issue_w(wissued[0])
                wissued[0] += 1
            return k % NWB

        bctr = [0]

        def next_bank():
            b = bctr[0] % 8
            bctr[0] += 1
            return b

        def next_pair():
            if bctr[0] % 2:
                bctr[0] += 1
            b = bctr[0] % 8
            bctr[0] += 2
            return b

        def mm_group(bank, bs, rhs_list, rhs_res):
            nk = len(rhs_list)

            def emit(e):
                ins = None
                for k in range(nk):
                    ins = e.matmul(ps[:, bank, :], wbf[bs][:, k, :], rhs_list[k], start=(k == 0), stop=(k == nk - 1))
                return ins
            S.op("pe", emit, reads=[("wbf", bs)] + list(rhs_res), writes=[("ps", bank)])

        def xbf_rhs(n):
            return [xbf[:, c, sub(n)] for c in range(DC)], [("xbf", c, n) for c in range(DC)]

        def down_proj(wname, k0, k1, first, chunks=None):
            for piece in down_proj_pieces(wname, k0, k1, first, chunks):
                piece()

        def down_proj_pieces(wname, k0, k1, first, chunks=None):
            kcn = k1 - k0
            if chunks is None:
                chunks = list(range(kcn))
            return [(lambda m=m: down_proj_m(wname, k0, k1, first, chunks, m)) for m in range(DC)]

        def down_proj_m(wname, k0, k1, first, chunks, m):
            kcn = k1 - k0
            if True:
                bs = load_w(wd[wname][m, :, k0:k1, :], kcn)
                for q in range(NT // 2):
                    bank = next_pair()
                    for r in range(2):
                        n = 2 * q + r
                        mm_group(bank + r, bs, [act[:, kk, sub(n)] for kk in chunks],
                                 [("act", kk, n) for kk in chunks])
                    xs = x32[:, m, 2 * q * TS:(2 * q + 2) * TS]
                    pin = psflat[:, bank * TS:(bank + 2) * TS]
                    rk = [("ps", bank), ("ps", bank + 1), ("x32", m, 2 * q), ("x32", m, 2 * q + 1)]
                    wk = [("x32", m, 2 * q), ("x32", m, 2 * q + 1)]
                    if first:
                        S.op("dve", (lambda xs=xs, pin=pin: (lambda e: e.scalar_tensor_tensor(
                            out=xs, in0=xs, scalar=ALPHA, in1=pin, op0=ALU.mult, op1=ALU.add)))(),
                            reads=rk, writes=wk)
                    else:
                        S.op("dve", (lambda xs=xs, pin=pin: (lambda e: e.tensor_tensor(
                            out=xs, in0=xs, in1=pin, op=ALU.add)))(),
                            reads=rk, writes=wk)

        def mm_kouter(items):
            for k in range(DC):
                def emit(e, k=k):
                    ins = None
                    for (bank, bs, n) in items:
                        ins = e.matmul(ps[:, bank, :], wbf[bs][:, k, :], xbf[:, k, sub(n)], start=(k == 0), stop=(k == DC - 1))
                    return ins
                S.op("pe", emit, reads=[("wbf", bs) for (_, bs, _) in items] + [("xbf", k, n) for (_, _, n) in items],
                     writes=[("ps", bank) for (bank, _, _) in items])

        def layer_norm(gname, bname, out_seq=None):
            W0, W1, WM, WR = 0, 1, 3, 4
            for c in range(DC):
                wb = work[c % 2]
                rb = wb[:, WPAD:WPAD + T // 2].bitcast(BF16)
                sq = wb[:, WPAD + T // 2:WPAD + T].bitcast(BF16)
                S.op("act", (lambda rb=rb, c=c: (lambda e: e.activation(out=rb, in_=x32[:, c, :], func=AF.Copy)))(),
                     reads=[("x32", c, n) for n in range(NT)], writes=[("work", c % 2)])
                S.op("dve", (lambda sq=sq, c=c: (lambda e: e.tensor_tensor(out=sq, in0=x32[:, c, :], in1=x32[:, c, :], op=ALU.mult)))(),
                     reads=[("x32", c, n) for n in range(NT)], writes=[("work", c % 2)])

                def emit(e, rb=rb, sq=sq, c=c):
                    ins = None
                    for n in range(NT):
                        e.matmul(ps[:, n, :], ones[:], rb[:, sub(n)], start=(c == 0), stop=(c == DC - 1))
                        ins = e.matmul(ps[:, 4 + n, :], ones[:], sq[:, sub(n)], start=(c == 0), stop=(c == DC - 1))
                    return ins
                S.op("pe", emit, reads=[("work", c % 2), ("cst",)], writes=[("ps", b) for b in range(8)])
            bctr[0] = 0
            mean = work[WM][:, WPAD:WPAD + T]
            rstd = work[WR][:, WPAD:WPAD + T]
            psk_s = [("ps", n) for n in range(4)]
            psk_q = [("ps", 4 + n) for n in range(4)]
            S.op("act", lambda e: e.activation(out=mean, in_=psflat[:, 0:T], func=AF.Copy, scale=1.0 / D),
                 reads=psk_s, writes=[("work", WM)])
            S.op("act", lambda e: e.activation(out=rstd, in_=psflat[:, 0:T], func=AF.Square, scale=1.0 / D),
                 reads=psk_s, writes=[("work", WR)])
            S.op("dve", lambda e: e.scalar_tensor_tensor(out=rstd, in0=psflat[:, T:2 * T], scalar=1.0 / D, in1=rstd,
                                                         op0=ALU.mult, op1=ALU.subtract),
                 reads=psk_q + [("work", WR)], writes=[("work", WR)])
            S.op("act", lambda e: e.activation(out=rstd, in_=rstd, func=AF.Ln, bias=kc[:, 2:3], scale=1.0),
                 reads=[("work", WR), ("cst",)], writes=[("work", WR)])
            S.op("act", lambda e: e.activation(out=rstd, in_=rstd, func=AF.Exp, scale=-0.5),
                 reads=[("work", WR)], writes=[("work", WR)])
            for c in range(DC):
                xr_ = [("x32", c, n) for n in range(NT)]
                eng = "dve"
                S.op(eng, (lambda c=c: (lambda e: e.tensor_tensor(out=x32[:, c, :], in0=x32[:, c, :], in1=mean, op=ALU.subtract)))(),
                     reads=xr_ + [("work", WM)], writes=xr_)
                S.op(eng, (lambda c=c: (lambda e: e.tensor_tensor(out=x32[:, c, :], in0=x32[:, c, :], in1=rstd, op=ALU.mult)))(),
                     reads=xr_ + [("work", WR)], writes=xr_)
                if out_seq is None:
                    S.op("act", (lambda c=c: (lambda e: e.activation(out=x32[:, c, :], in_=x32[:, c, :], func=AF.Identity,
                                                                  bias=pcol(bname, c), scale=pcol(gname, c))))(),
                         reads=xr_ + RPAR, writes=xr_)
                    S.op("act", (lambda c=c: (lambda e: e.activation(out=xbf[:, c, :], in_=x32[:, c, :], func=AF.Copy)))(),
                         reads=xr_, writes=[("xbf", c, n) for n in range(NT)])
                else:
                    ob = c % 3
                    S.op("act", (lambda c=c, ob=ob: (lambda e: e.activation(out=work[ob][:, WPAD:WPAD + T], in_=x32[:, c, :], func=AF.Identity,
                                                                         bias=pcol(bname, c), scale=pcol(gname, c))))(),
                         reads=xr_ + RPAR, writes=[("work", ob)])
                    S.op("sp", (lambda c=c, ob=ob: (lambda e, sm: e.dma_start(out=yout[out_seq, :, c, :], in_=work[ob][:, WPAD:WPAD + T]).then_inc(sm, 16)))(),
                         reads=[("work", ob)], writes=[("yout", ob)], dma_sem=f"out{ob}")

        def sc_mixer(i):
            WV, WB, WC, WU = 0, 1, 2, 3
            for j in range(DC):
                pre = {}
                if j == 0:
                    bctr[0] = 0
                    items = []
                    for ti, (mchunk, dst) in enumerate(((16 + j, WV), (j, WB))):
                        bs_ = load_w(wd[f"win{i}"][mchunk, :, :, :], 8, LOOKAHEAD - ti)
                        for n in range(NT):
                            bank_ = next_bank()
                            pre[(dst, n)] = bank_
                            items.append((bank_, bs_, n))
                    mm_kouter(items)
                for (mchunk, dst) in ((16 + j, WV), (j, WB)):
                    if j > 0:
                        bs = load_w(wd[f"win{i}"][mchunk, :, :, :], 8)
                    for n in range(NT):
                        if j == 0:
                            bank = pre[(dst, n)]
                        else:
                            bank = next_bank()
                            rl, rr = xbf_rhs(n)
                            mm_group(bank, bs, rl, rr)
                        S.op("act", (lambda dst=dst, n=n, bank=bank: (lambda e: e.activation(
                            out=work[dst][:, WPAD + n * TS:WPAD + (n + 1) * TS], in_=ps[:, bank, :], func=AF.Copy)))(),
                            reads=[("ps", bank)], writes=[("work", dst)])
                bs = load_w(wd[f"win{i}"][8 + j, :, :, :], 8)
                for n in range(NT):
                    bank = next_bank()
                    rl, rr = xbf_rhs(n)
                    mm_group(bank, bs, rl, rr)
                    S.op("dve", (lambda n=n, bank=bank: (lambda e: e.tensor_tensor(
                        out=work[WC][:, WPAD + n * TS:WPAD + (n + 1) * TS], in0=ps[:, bank, :],
                        in1=work[WV][:, WPAD + n * TS:WPAD + (n + 1) * TS], op=ALU.mult)))(),
                        reads=[("ps", bank), ("work", WV)], writes=[("work", WC)])
                cv = work[WC]
                u = work[WU]
                S.op("dve", (lambda j=j: (lambda e: e.tensor_scalar(
                    out=u[:, WPAD:WPAD + T], in0=cv[:, WPAD:WPAD + T], scalar1=pcol(f"scw{i}_2", j), scalar2=pcol(f"scb{i}", j),
                    op0=ALU.mult, op1=ALU.add)))(),
                    reads=[("work", WC)] + RPAR, writes=[("work", WU)])
                for sh, kk in ((1, 1), (2, 0)):
                    S.op("dve", (lambda j=j, sh=sh, kk=kk: (lambda e: e.scalar_tensor_tensor(
                        out=u[:, WPAD:WPAD + T], in0=cv[:, WPAD - sh:WPAD - sh + T], scalar=pcol(f"scw{i}_{kk}", j),
                        in1=u[:, WPAD:WPAD + T], op0=ALU.mult, op1=ALU.add)))(),
                        reads=[("work", WC), ("work", WU)] + RPAR, writes=[("work", WU)])
                S.op("dve", (lambda j=j: (lambda e: e.tensor_tensor(
                    out=act[:, j, :], in0=u[:, WPAD:WPAD + T], in1=work[WB][:, WPAD:WPAD + T], op=ALU.mult)))(),
                    reads=[("work", WU), ("work", WB)], writes=[("act", j, n) for n in range(NT)])
            down_proj(f"wout{i}", 0, 8, True)

        def lru_mixer(i):
            NU = 4
            HH = T // NU
            HB = WPAD + HH

            def hw(sx, b):
                return work[b][:, sx * HB:(sx + 1) * HB]

            def Dv(sx, b):
                return work[b][:, sx * HB + WPAD:sx * HB + WPAD + HH]

            def K(sx, b):
                return [("work", b, sx)]

            XK = [[("xrbf", u)] for u in range(NU)]
            pending = []
            for grp, (h0, h1) in enumerate(((0, 5), (5, 10))):
                for h in range(h0, h1):
                    pre = {}
                    if h == 0:
                        bctr[0] = 0
                        items = []
                        for ti, (mchunk, dstb) in enumerate(((RC + h, 0), (h, 4))):
                            bs_ = load_w(wd[f"win{i}"][mchunk, :, :, :], 8, LOOKAHEAD - ti)
                            for sx in range(NU):
                                bank_ = next_bank()
                                pre[(dstb, sx)] = bank_
                                items.append((bank_, bs_, sx))
                        mm_kouter(items)
                    for (mchunk, dstb, bcol, fn) in ((RC + h, 0, RC + h, AF.Identity), (h, 4, h, AF.Gelu_apprx_tanh)):
                        if h > 0:
                            bs = load_w(wd[f"win{i}"][mchunk, :, :, :], 8)
                        for sx in range(NU):
                            if h == 0:
                                bank = pre[(dstb, sx)]
                            else:
                                bank = next_bank()
                                rl, rr = xbf_rhs(sx)
                                mm_group(bank, bs, rl, rr)
                            S.op("act", (lambda sx=sx, bank=bank, dstb=dstb, bcol=bcol, fn=fn: (lambda e: e.activation(
                                out=Dv(sx, dstb), in_=ps[:, bank, :], func=fn,
                                bias=pcol(f"lbin{i}", bcol), scale=1.0)))(),
                                reads=[("ps", bank)] + RPAR, writes=K(sx, dstb))
                    for sx in range(1, NU):
                        S.op("dve", (lambda sx=sx: (lambda e: e.tensor_copy(out=hw(sx, 0)[:, WPAD - 3:WPAD], in_=hw(sx - 1, 0)[:, HB - 3:HB])))(),
                             reads=K(sx - 1, 0), writes=K(sx, 0))
                    bsg = load_w(wd[f"wg{i}"][h, :, :, :], 2)
                    for sx in range(NU):
                        S.op("dve", (lambda sx=sx, h=h: (lambda e: e.tensor_scalar(
                            out=Dv(sx, 1), in0=Dv(sx, 0), scalar1=pcol(f"lcw{i}_3", h), scalar2=pcol(f"lcb{i}", h),
                            op0=ALU.mult, op1=ALU.add)))(),
                            reads=K(sx, 0) + RPAR, writes=K(sx, 1))
                        for sh, kk in ((1, 2), (2, 1), (3, 0)):
                            S.op("dve", (lambda sx=sx, h=h, sh=sh, kk=kk: (lambda e: e.scalar_tensor_tensor(
                                out=Dv(sx, 1), in0=hw(sx, 0)[:, WPAD - sh:WPAD - sh + HH], scalar=pcol(f"lcw{i}_{kk}", h),
                                in1=Dv(sx, 1), op0=ALU.mult, op1=ALU.add)))(),
                                reads=K(sx, 0) + K(sx, 1) + RPAR, writes=K(sx, 1))
                        S.op("act", (lambda sx=sx: (lambda e: e.activation(out=xrbf[:, sx * HH:(sx + 1) * HH], in_=Dv(sx, 1), func=AF.Copy)))(),
                             reads=K(sx, 1), writes=XK[sx])
                        for g in range(2):
                            bank = next_bank()

                            def emit(e, bank=bank, bsg=bsg, g=g, sx=sx):
                                return e.matmul(ps[:, bank, :], wbf[bsg][:, g, :], xrbf[:, sx * HH:(sx + 1) * HH], start=True, stop=True)
                            S.op("pe", emit, reads=[("wbf", bsg)] + XK[sx], writes=[("ps", bank)])
                            dstb = 2 if g == 0 else 0
                            S.op("act", (lambda h=h, g=g, sx=sx, bank=bank, dstb=dstb: (lambda e: e.activation(
                                out=Dv(sx, dstb), in_=ps[:, bank, :], func=AF.Tanh,
                                bias=pcol(f"lhbg{i}", g * RC + h), scale=0.5)))(),
                                reads=[("ps", bank)] + RPAR, writes=K(sx, dstb))
                        S.op("act", (lambda sx=sx, h=h: (lambda e: e.activation(out=Dv(sx, 3), in_=Dv(sx, 2), func=AF.Exp,
                                                                            bias=pcol(f"lhc{i}", h), scale=pcol(f"lhc{i}", h))))(),
                             reads=K(sx, 2) + RPAR, writes=K(sx, 3))
                        S.op("act", (lambda sx=sx, h=h: (lambda e: e.activation(out=Dv(sx, 2), in_=Dv(sx, 2), func=AF.Exp,
                                                                            bias=pcol(f"lc{i}", h), scale=pcol(f"lc{i}", h))))(),
                             reads=K(sx, 2) + RPAR, writes=K(sx, 2))
                    for sx in range(NU):
                        S.op("dve", (lambda sx=sx: (lambda e: e.tensor_scalar(out=Dv(sx, 2), in0=Dv(sx, 2),
                                                                           scalar1=0.9999999, scalar2=-1.0, op0=ALU.min, op1=ALU.mult)))(),
                             reads=K(sx, 2), writes=K(sx, 2))
                    for sx in range(NU):
                        S.op("act", (lambda sx=sx: (lambda e: e.activation(out=Dv(sx, 2), in_=Dv(sx, 2), func=AF.Sqrt,
                                                                      bias=kc[:, 1:2], scale=1.0)))(),
                             reads=K(sx, 2) + [("cst",)], writes=K(sx, 2))
                    for sx in range(NU):
                        S.op("dve", (lambda sx=sx: (lambda e: e.scalar_tensor_tensor(
                            out=Dv(sx, 0), in0=Dv(sx, 0), scalar=1.0, in1=Dv(sx, 1), op0=ALU.add, op1=ALU.mult)))(),
                            reads=K(sx, 0) + K(sx, 1), writes=K(sx, 0))
                        S.op("dve", (lambda sx=sx: (lambda e: e.scalar_tensor_tensor(
                            out=Dv(sx, 0), in0=Dv(sx, 0), scalar=0.5, in1=Dv(sx, 2), op0=ALU.mult, op1=ALU.mult)))(),
                            reads=K(sx, 0) + K(sx, 2), writes=K(sx, 0))
                        if sx == 0:
                            S.op("dve", (lambda: (lambda e: e.tensor_tensor_scan(
                                out=Dv(0, 1), data0=Dv(0, 3), data1=Dv(0, 0), initial=0.0, op0=ALU.mult, op1=ALU.add)))(),
                                reads=K(0, 3) + K(0, 0), writes=K(0, 1))
                        else:
                            S.op("dve", (lambda sx=sx: (lambda e: e.tensor_tensor_scan(
                                out=Dv(sx, 1), data0=Dv(sx, 3), data1=Dv(sx, 0), initial=Dv(sx - 1, 1)[:, HH - 1:HH],
                                op0=ALU.mult, op1=ALU.add)))(),
                                reads=K(sx, 3) + K(sx, 0) + K(sx - 1, 1), writes=K(sx, 1))
                        S.op("dve", (lambda sx=sx, h=h, h0=h0: (lambda e: e.tensor_tensor(
                            out=act[:, h % 8, sx * HH:(sx + 1) * HH], in0=Dv(sx, 4), in1=Dv(sx, 1), op=ALU.mult)))(),
                            reads=K(sx, 4) + K(sx, 1), writes=[("act", h % 8, sx)])
                    for _ in range(3):
                        if pending:
                            pending.pop(0)()
                cl = [hh % 8 for hh in range(h0, h1)]
                if grp == 0:
                    pending = down_proj_pieces(f"wout{i}", h0, h1, True, cl)
                else:
                    while pending:
                        pending.pop(0)()
                    down_proj(f"wout{i}", h0, h1, False, cl)

        def ffn(i):
            groups = ((0, 8), (8, 15), (15, 22))

            def pair_front(j):
                par_ = j % 2
                if j == 0:
                    items = []
                    for ti, (mchunk, base) in enumerate(((j, 0), (FC + j, 4))):
                        bs_ = load_w(wd[f"wup{i}"][mchunk, :, :, :], 8, LOOKAHEAD - ti)
                        items += [(base + n, bs_, n) for n in range(NT)]
                    mm_kouter(items)
                for (mchunk, dstb, base) in ((j, 2 * par_, 0), (FC + j, 2 * par_ + 1, 4)):
                    dst = work[dstb]
                    if j > 0:
                        bs = load_w(wd[f"wup{i}"][mchunk, :, :, :], 8)
                        for n in range(NT):
                            rl, rr = xbf_rhs(n)
                            mm_group(base + n, bs, rl, rr)
                    pk = [("ps", base + n) for n in range(NT)]
                    S.op("act", (lambda mchunk=mchunk, base=base, dst=dst: (lambda e: e.activation(
                        out=dst[:, WPAD:WPAD + T], in_=psflat[:, base * TS:base * TS + T], func=AF.Identity,
                        bias=pcol(f"fcb{i}", mchunk), scale=pcol(f"fcw{i}_2", mchunk))))(),
                        reads=pk + RPAR, writes=[("work", dstb)])
                    for sh, kk in ((1, 1), (2, 0)):
                        S.op("dve", (lambda mchunk=mchunk, kk=kk, sh=sh, base=base, dst=dst: (lambda e: e.scalar_tensor_tensor(
                            out=dst[:, WPAD + sh:WPAD + T], in0=psflat[:, base * TS:base * TS + T - sh],
                            scalar=pcol(f"fcw{i}_{kk}", mchunk), in1=dst[:, WPAD + sh:WPAD + T],
                            op0=ALU.mult, op1=ALU.add)))(),
                            reads=pk + [("work", dstb)] + RPAR, writes=[("work", dstb)])

            def pair_back(j, j0):
                par_ = j % 2
                cg = work[2 * par_]
                cvv = work[2 * par_ + 1]
                S.op("act", (lambda cg=cg: (lambda e: e.activation(out=cg[:, WPAD:WPAD + T], in_=cg[:, WPAD:WPAD + T], func=AF.Silu)))(),
                     reads=[("work", 2 * par_)], writes=[("work", 2 * par_)])
                S.op("dve", (lambda cg=cg, cvv=cvv, j=j, j0=j0: (lambda e: e.tensor_tensor(
                    out=act[:, j - j0, :], in0=cg[:, WPAD:WPAD + T], in1=cvv[:, WPAD:WPAD + T], op=ALU.mult)))(),
                    reads=[("work", 2 * par_), ("work", 2 * par_ + 1)],
                    writes=[("act", j - j0, n) for n in range(NT)])

            for grp, (j0, j1) in enumerate(groups):
                for j in range(j0, j1):
                    if not (grp > 0 and j == j0):
                        pair_front(j)
                    pair_back(j, j0)
                if grp + 1 < len(groups):
                    pair_front(groups[grp + 1][0])
                bctr[0] = 0
                down_proj(f"wdn{i}", j0, j1, grp == 0)

        def work_barrier(idxs):
            pass

        def main_ops():
            for s in range(nseq):
                for c in range(DC):
                    S.op("sp", (lambda s=s, c=c: (lambda e, sm: e.dma_start(out=x32[:, c, :], in_=xin[s, :, c, :]).then_inc(sm, 16)))(),
                         writes=[("x32", c, n) for n in range(NT)], dma_sem=f"xin{c}")
                for c in range(DC):
                    S.op("act", (lambda c=c: (lambda e: e.activation(out=xbf[:, c, :], in_=x32[:, c, :], func=AF.Copy)))(),
                         reads=[("x32", c, n) for n in range(NT)], writes=[("xbf", c, n) for n in range(NT)])
                for li, i in enumerate(layers):
                    if i % 2 == 0:
                        sc_mixer(i)
                    else:
                        lru_mixer(i)
                    layer_norm(f"lng{i}_0", f"lnb{i}_0")
                    ffn(i)
                    last = (li == len(layers) - 1)
                    layer_norm(f"lng{i}_1", f"lnb{i}_1", s if last else None)
            S.op("sp", None, reads=[("yout", 0), ("yout", 1), ("yout", 2)])
        collect[0] = True
        setup_ops()
        main_ops()
        collect[0] = False
        S = Sched()
        wctr[0] = 0
        wissued[0] = 0
        bctr[0] = 0
        setup_ops()
        main_ops()
        S.finalize()

        engmap = {"sp": block.sync, "pe": block.tensor, "act": block.scalar, "dve": block.vector, "pool": block.gpsimd}

        @block.sync
        def _(e):
            S.emit_engine("sp", e, sems, dsems)

        @block.tensor
        def _(e):
            S.emit_engine("pe", e, sems, dsems)

        @block.scalar
        def _(e):
            S.emit_engine("act", e, sems, dsems)

        @block.vector
        def _(e):
            S.emit_engine("dve", e, sems, dsems)

        @block.gpsimd
        def _(e):
            S.emit_engine("pool", e, sems, dsems)
    return nc


def prep_weights(inp, layers):
    w = {}
    for i in layers:
        j = i // 2
        if i % 2 == 0:
            w[f"win{i}"] = _wtiles(np.asarray(inp["sc_w_in"][j], np.float32))
            w[f"wout{i}"] = _wtiles(np.asarray(inp["sc_w_out"][j], np.float32))
        else:
            w[f"win{i}"] = _wtiles(np.asarray(inp["lru_w_in"][j], np.float32))
            w[f"wg{i}"] = np.ascontiguousarray(np.asarray(inp["lru_w_gate"][j], np.float32).reshape(10, 128, 2, 128))
            w[f"wout{i}"] = _wtiles(np.asarray(inp["lru_w_out"][j], np.float32))
        w[f"wup{i}"] = _wtiles(np.asarray(inp["ffn_w_up"][i], np.float32))
        w[f"wdn{i}"] = _wtiles(np.asarray(inp["ffn_w_down"][i], np.float32))
    return w


def run(inp, layers, nseq, ncores=NCORES, trace=False):
    x = np.asarray(inp["x"], np.float32)
    PL = build_params(inp, layers)
    par = PL.build()
    w = prep_weights(inp, layers)
    nc = build_nc(nseq, layers, PL)
    in_maps = []
    for c in range(ncores):
        xs = x[c * nseq:(c + 1) * nseq]
        xT = np.ascontiguousarray(xs.reshape(nseq, T, DC, 128).transpose(0, 3, 2, 1))
        m = {"xin": xT, "par": par}
        m.update(w)
        in_maps.append(m)
    res = run_bass_kernel_spmd(nc, in_maps, core_ids=list(range(ncores)), trace=trace)
    outs = []
    for c in range(ncores):
        y = res.results[c]["yout"]
        outs.append(np.ascontiguousarray(y.transpose(0, 3, 2, 1)).reshape(nseq, T, D))
    return np.concatenate(outs, axis=0), res


def kernel(**inputs):
    out, _ = run(inputs, list(range(DEPTH)), 4)
    return out.astype(np.float32)
```
